# Optimizing a Trainium2 kernel written in Bass

```python
import jax
import jax.numpy as jnp
from jax import lax
import numpy as np


D_MODEL = 2048
BATCH = 1
SEQ = 8192
DEPTH = 2

CTX_LEN = 256
GRID_W = 64
Q_BLOCK = 128
ROPE_BASE = 10000.0
EPS = 1e-6
N_BRANCH = 4
BRANCH_W = 1024
CONV_W = 1024
CONV_K = 3
GQA_HEADS = 8
GQA_KV_HEADS = 2
GQA_GROUP = GQA_HEADS // GQA_KV_HEADS
GQA_HEAD_DIM = 128
GQA_KV_W = GQA_KV_HEADS * GQA_HEAD_DIM
MLA_HEADS = 8
MLA_Q_RANK = 512
MLA_KV_RANK = 512
MLA_NOPE = 128
MLA_ROPE = 64
MLA_V = 128
SGU_W = 1024
SGU_GROUPS = 8
SGU_CHUNK = 128
D_FF = 4 * D_MODEL

KV_COLS = 2 * GQA_KV_W + MLA_KV_RANK + MLA_ROPE
Q_COLS = GQA_HEADS * GQA_HEAD_DIM + MLA_Q_RANK
A_COLS = 3 * CONV_W
D_COLS = 2 * SGU_W
G_COLS = N_BRANCH * D_MODEL
IN_COLS = KV_COLS + Q_COLS + A_COLS + D_COLS + G_COLS
IN_SPLITS = (KV_COLS, KV_COLS + Q_COLS, KV_COLS + Q_COLS + A_COLS, KV_COLS + Q_COLS + A_COLS + D_COLS)
KV_SPLITS = (GQA_KV_W, 2 * GQA_KV_W, 2 * GQA_KV_W + MLA_KV_RANK)
Q_SPLITS = (GQA_HEADS * GQA_HEAD_DIM,)

kernel_name = 'hybrid_parallel_dit_block'


def rms_norm(x, g):
    xf = x.astype(jnp.float32)
    y = xf * lax.rsqrt(jnp.mean(xf * xf, axis=-1, keepdims=True) + EPS)
    return (y * g.astype(jnp.float32)).astype(x.dtype)


def layer_norm(x, g, b):
    xf = x.astype(jnp.float32)
    mu = jnp.mean(xf, axis=-1, keepdims=True)
    xc = xf - mu
    y = xc * lax.rsqrt(jnp.mean(xc * xc, axis=-1, keepdims=True) + EPS)
    return (y * g.astype(jnp.float32) + b.astype(jnp.float32)).astype(x.dtype)


def modulate(x, g, shift, scale):
    return rms_norm(x, g) * (1 + scale) + shift


def adaln(cond, w, b):
    m = jax.nn.silu(cond) @ w + b
    if m.ndim == 2:
        m = m[:, None, :]
    return jnp.split(m, 6, axis=-1)


def axial_angles(n_tok, rot_dim):
    n_rows = n_tok // GRID_W
    row = jnp.repeat(jnp.arange(n_rows), GRID_W).astype(jnp.float32)
    col = jnp.tile(jnp.arange(GRID_W), n_rows).astype(jnp.float32)
    axis_dim = rot_dim // 2
    inv = ROPE_BASE ** (-jnp.arange(0, axis_dim, 2, dtype=jnp.float32) / axis_dim)
    return row[:, None] * inv, col[:, None] * inv


def rope_axis(x, ang):
    cos = jnp.cos(ang)[None, :, None, :].astype(x.dtype)
    sin = jnp.sin(ang)[None, :, None, :].astype(x.dtype)
    x1, x2 = jnp.split(x, 2, axis=-1)
    return jnp.concatenate([x1 * cos - x2 * sin, x1 * sin + x2 * cos], axis=-1)


def axial_rope(x, ang_row, ang_col):
    x_row, x_col = jnp.split(x, 2, axis=-1)
    return jnp.concatenate([rope_axis(x_row, ang_row), rope_axis(x_col, ang_col)], axis=-1)


def block_attention(q, k, v):
    B, S = q.shape[0], q.shape[1]
    n_blk = S // Q_BLOCK
    scale = q.shape[-1] ** -0.5
    qb = q.reshape((B, n_blk, Q_BLOCK) + q.shape[2:]).swapaxes(0, 1)

    def one_block(q_blk):
        s = jnp.einsum('bqhgd,bthd->bhgqt', q_blk, k).astype(jnp.float32) * scale
        p = jax.nn.softmax(s, axis=-1).astype(v.dtype)
        return jnp.einsum('bhgqt,bthe->bqhge', p, v)

    out = lax.map(one_block, qb)
    return out.swapaxes(0, 1).reshape((B, S) + out.shape[3:])


def short_conv(x, w):
    S = x.shape[1]
    pad = CONV_K // 2
    xp = jnp.pad(x, ((0, 0), (pad, pad), (0, 0)))
    out = xp[:, 0:S] * w[0]
    for tap in range(1, CONV_K):
        out = out + xp[:, tap:tap + S] * w[tap]
    return out


def spatial_gating(u, v, ln_g, ln_b, w_s, b_s):
    B, S, _ = v.shape
    n_chunk = S // SGU_CHUNK
    v = layer_norm(v, ln_g, ln_b)
    vc = v.reshape(B, n_chunk, SGU_CHUNK, SGU_GROUPS, SGU_W // SGU_GROUPS)
    mixed = jnp.einsum('gpq,bnqgc->bnpgc', w_s, vc) + b_s.T[:, :, None]
    return u * mixed.reshape(B, S, SGU_W)


def attn_kv(p_kv, lp, rope_b, rope_c):
    B, S, _ = p_kv.shape
    k_b, v_b, c_kv, k_r = jnp.split(p_kv, KV_SPLITS, axis=-1)
    k_b = rms_norm(k_b.reshape(B, S, GQA_KV_HEADS, GQA_HEAD_DIM), lp['k_norm_g'])
    v_b = v_b.reshape(B, S, GQA_KV_HEADS, GQA_HEAD_DIM)
    c_kv = rms_norm(c_kv, lp['mla_kv_norm_g'])
    kv = (c_kv @ lp['w_ukv']).reshape(B, S, MLA_HEADS, MLA_NOPE + MLA_V)
    k_nope, v_c = jnp.split(kv, (MLA_NOPE,), axis=-1)
    k_r = k_r[:, :, None, :]
    if rope_b is not None:
        k_b = axial_rope(k_b, *rope_b)
        k_r = axial_rope(k_r, *rope_c)
    k_c = jnp.concatenate([k_nope, jnp.broadcast_to(k_r, (B, S, MLA_HEADS, MLA_ROPE))], axis=-1)
    return k_b, v_b, k_c, v_c


def attn_q(p_q, lp, rope_b, rope_c):
    B, S, _ = p_q.shape
    q_b, c_q = jnp.split(p_q, Q_SPLITS, axis=-1)
    q_b = rms_norm(q_b.reshape(B, S, GQA_HEADS, GQA_HEAD_DIM), lp['q_norm_g'])
    c_q = rms_norm(c_q, lp['mla_q_norm_g'])
    q_c = (c_q @ lp['w_uq']).reshape(B, S, MLA_HEADS, MLA_NOPE + MLA_ROPE)
    q_nope, q_rope = jnp.split(q_c, (MLA_NOPE,), axis=-1)
    if rope_b is not None:
        q_b = axial_rope(q_b, *rope_b)
        q_rope = axial_rope(q_rope, *rope_c)
    q_b = q_b.reshape(B, S, GQA_KV_HEADS, GQA_GROUP, GQA_HEAD_DIM)
    q_c = jnp.concatenate([q_nope, q_rope], axis=-1)[:, :, :, None, :]
    return q_b, q_c


def token_mixer(h, lp, rope_b, rope_c, ctx_kv):
    B, S, _ = h.shape
    p = h @ lp['w_in']
    p_kv, p_q, p_a, p_d, p_g = jnp.split(p, IN_SPLITS, axis=-1)
    kv_self = attn_kv(p_kv, lp, rope_b, rope_c)
    if ctx_kv is None:
        k_b, v_b, k_c, v_c = kv_self
    else:
        k_b, v_b, k_c, v_c = [jnp.concatenate([kc, ks], axis=1) for kc, ks in zip(ctx_kv, kv_self)]
    q_b, q_c = attn_q(p_q, lp, rope_b, rope_c)
    y_b = block_attention(q_b, k_b, v_b).reshape(B, S, BRANCH_W)
    y_c = block_attention(q_c, k_c, v_c).reshape(B, S, BRANCH_W)
    gate_b, gate_c, x_a = jnp.split(p_a, 3, axis=-1)
    y_a = gate_b * short_conv(gate_c * x_a, lp['conv_w'])
    u, v = jnp.split(jax.nn.gelu(p_d), 2, axis=-1)
    y_d = spatial_gating(u, v, lp['sgu_ln_g'], lp['sgu_ln_b'], lp['sgu_w_s'], lp['sgu_b_s'])
    gates = jax.nn.sigmoid(p_g + lp['b_gate']).reshape(B, S, N_BRANCH, D_MODEL)
    ys = jnp.stack([y_a, y_b, y_c, y_d], axis=2)
    branch = jnp.einsum('bsie,ied->bsid', ys, lp['w_branch'])
    merged = jnp.sum(gates * branch, axis=2)
    return merged @ lp['w_out'], kv_self


def ffn(h, w1, w2):
    return jnp.square(jax.nn.relu(h @ w1)) @ w2


def setup_inputs(seed: int = 0) -> dict:
    key = jax.random.key(seed)
    ks = jax.random.split(key, 26)

    def nrm(i, shape, scale):
        return jax.random.normal(ks[i], shape, jnp.float32) * scale

    def gain(i, shape):
        return 1.0 + 0.02 * jax.random.normal(ks[i], shape, jnp.float32)

    L = DEPTH
    return {
        'x': nrm(0, (BATCH, SEQ, D_MODEL), 1.0),
        'c': nrm(1, (BATCH, D_MODEL), 1.0),
        'ctx': nrm(2, (BATCH, CTX_LEN, D_MODEL), 1.0),
        'c_ctx': nrm(3, (D_MODEL,), 1.0),
        'w_ada': nrm(4, (L, D_MODEL, 6 * D_MODEL), D_MODEL ** -0.5),
        'b_ada': nrm(5, (L, 6 * D_MODEL), 0.02),
        'norm_mix_g': gain(6, (L, D_MODEL)),
        'w_in': nrm(7, (L, D_MODEL, IN_COLS), D_MODEL ** -0.5),
        'b_gate': nrm(8, (L, G_COLS), 0.02),
        'conv_w': nrm(9, (L, CONV_K, CONV_W), CONV_K ** -0.5),
        'q_norm_g': gain(10, (L, GQA_HEAD_DIM)),
        'k_norm_g': gain(11, (L, GQA_HEAD_DIM)),
        'mla_q_norm_g': gain(12, (L, MLA_Q_RANK)),
        'mla_kv_norm_g': gain(13, (L, MLA_KV_RANK)),
        'w_uq': nrm(14, (L, MLA_Q_RANK, MLA_HEADS * (MLA_NOPE + MLA_ROPE)), MLA_Q_RANK ** -0.5),
        'w_ukv': nrm(15, (L, MLA_KV_RANK, MLA_HEADS * (MLA_NOPE + MLA_V)), MLA_KV_RANK ** -0.5),
        'sgu_ln_g': gain(16, (L, SGU_W)),
        'sgu_ln_b': nrm(17, (L, SGU_W), 0.02),
        'sgu_w_s': nrm(18, (L, SGU_GROUPS, SGU_CHUNK, SGU_CHUNK), SGU_CHUNK ** -0.5),
        'sgu_b_s': nrm(19, (L, SGU_GROUPS, SGU_CHUNK), 0.02),
        'w_branch': nrm(20, (L, N_BRANCH, BRANCH_W, D_MODEL), BRANCH_W ** -0.5),
        'w_out': nrm(21, (L, D_MODEL, D_MODEL), D_MODEL ** -0.5),
        'norm_ffn_g': gain(22, (L, D_MODEL)),
        'w_ff1': nrm(23, (L, D_MODEL, D_FF), D_MODEL ** -0.5),
        'w_ff2': nrm(24, (L, D_FF, D_MODEL), D_FF ** -0.5),
        'final_norm_g': gain(25, (D_MODEL,)),
    }


def reference(x, c, ctx, c_ctx, w_ada, b_ada, norm_mix_g, w_in, b_gate, conv_w, q_norm_g, k_norm_g,
              mla_q_norm_g, mla_kv_norm_g, w_uq, w_ukv, sgu_ln_g, sgu_ln_b, sgu_w_s, sgu_b_s,
              w_branch, w_out, norm_ffn_g, w_ff1, w_ff2, final_norm_g):
    S = x.shape[1]
    rope_b = axial_angles(S, GQA_HEAD_DIM)
    rope_c = axial_angles(S, MLA_ROPE)
    z = ctx
    for l in range(DEPTH):
        lp = {
            'w_in': w_in[l], 'b_gate': b_gate[l], 'conv_w': conv_w[l],
            'q_norm_g': q_norm_g[l], 'k_norm_g': k_norm_g[l],
            'mla_q_norm_g': mla_q_norm_g[l], 'mla_kv_norm_g': mla_kv_norm_g[l],
            'w_uq': w_uq[l], 'w_ukv': w_ukv[l],
            'sgu_ln_g': sgu_ln_g[l], 'sgu_ln_b': sgu_ln_b[l], 'sgu_w_s': sgu_w_s[l], 'sgu_b_s': sgu_b_s[l],
            'w_branch': w_branch[l], 'w_out': w_out[l],
        }
        sh1, sc1, g1, sh2, sc2, g2 = adaln(c, w_ada[l], b_ada[l])
        csh1, csc1, cg1, csh2, csc2, cg2 = adaln(c_ctx, w_ada[l], b_ada[l])
        hz = modulate(z, norm_mix_g[l], csh1, csc1)
        last = l == DEPTH - 1
        if last:
            ctx_kv = attn_kv(hz @ w_in[l][:, :KV_COLS], lp, None, None)
        else:
            mz, ctx_kv = token_mixer(hz, lp, None, None, None)
        hx = modulate(x, norm_mix_g[l], sh1, sc1)
        mx, _ = token_mixer(hx, lp, rope_b, rope_c, ctx_kv)
        x = x + g1 * mx
        x = x + g2 * ffn(modulate(x, norm_ffn_g[l], sh2, sc2), w_ff1[l], w_ff2[l])
        if not last:
            z = z + cg1 * mz
            z = z + cg2 * ffn(modulate(z, norm_ffn_g[l], csh2, csc2), w_ff1[l], w_ff2[l])
    return rms_norm(x, final_norm_g)
```

```python
import contextlib
import numpy as np
import ml_dtypes
import concourse.bass as bass
import concourse.mybir as mybir
from concourse.bass_utils import run_bass_kernel_spmd

F32 = mybir.dt.float32
BF16 = mybir.dt.bfloat16
ALU = mybir.AluOpType
AF = mybir.ActivationFunctionType

NCORES = 8
D = 2048
SEQ = 8192
TX = SEQ // NCORES
TC = 256
DFF = 8192
EPS = 1e-6
IN_COLS = 15936
C_KB, C_VB, C_CKV, C_KR, C_QB, C_CQ = 0, 256, 512, 1024, 1088, 2112
C_A, C_D, C_G = 2624, 5696, 7744
KROWS = 1408
VROW0 = KROWS
HROW0 = KROWS + 1280
ROWS = HROW0 + 2
NP = 256
P_BADA, P_GMIX, P_GFFN, P_BGATE, P_CONV, P_QG, P_KG, P_MQG, P_MKG, P_FIN = 0, 96, 112, 128, 192, 216, 217, 218, 222, 226

ENGS = ("pe", "act", "dve", "pool", "sp")


class Buf:
    __slots__ = ("name", "w", "r", "pre", "excl")

    def __init__(self, name, excl=False):
        self.name = name
        self.w = None
        self.r = {}
        self.pre = {}
        self.excl = excl


class Sched:
    def __init__(self, nc):
        self.nc = nc
        self.q = {e: [] for e in ENGS}
        self.cnt = {}
        self.waited = {e: {} for e in ENGS}
        self.dma_keys = []
        self.force_q = None

    def _deps(self, eng, reads, wfull, wpart):
        need = {}

        def add(tok):
            if tok:
                for k, v in tok.items():
                    if v > need.get(k, 0):
                        need[k] = v
        for b in reads:
            add(b.w)
        for b in wfull:
            add(b.r)
            add(b.w)
        for b in wpart:
            add(b.r)
            add(b.pre)
        out = []
        for k, v in need.items():
            if k == eng and eng == "pe":
                continue
            if self.waited[eng].get(k, 0) >= v:
                continue
            self.waited[eng][k] = v
            out.append((k, v))
        return out

    def _mark(self, tok, reads, wfull, wpart):
        for b in reads:
            for k, v in tok.items():
                if v > b.r.get(k, 0):
                    b.r[k] = v
        for b in wpart:
            if b.w is not None:
                nw = dict(b.w)
                for k, v in tok.items():
                    if v > nw.get(k, 0):
                        nw[k] = v
                b.w = nw
            else:
                b.w = dict(tok)
        for b in wfull:
            pre = dict(b.r)
            if b.w:
                for k, v in b.w.items():
                    if v > pre.get(k, 0):
                        pre[k] = v
            b.pre = pre
            b.w = dict(tok)
            b.r = {}

    @staticmethod
    def _split(reads, writes, part):
        rd = [b for b in reads if not b.excl]
        ex = [b for b in reads if b.excl]
        if part:
            return rd, ex, list(writes)
        return rd, ex + list(writes), []

    def op(self, eng, fn, reads=(), writes=(), part=False):
        rd, wf, wp = self._split(reads, writes, part)
        waits = self._deps(eng, rd, wf, wp)
        self.cnt[eng] = self.cnt.get(eng, 0) + 1
        self.q[eng].append((waits, fn, eng, 1))
        self._mark({eng: self.cnt[eng]}, rd, wf, wp)

    def dma(self, qeng, fn, reads=(), writes=(), key=None, part=False, inc=16):
        if self.force_q is not None:
            qeng = self.force_q
        if key is None:
            key = "d:" + (writes[0].name if writes else reads[0].name)
        if key not in self.cnt:
            self.cnt[key] = 0
            self.dma_keys.append(key)
        rd, wf, wp = self._split(reads, writes, part)
        waits = self._deps(qeng, rd, wf, wp)
        self.cnt[key] += inc
        self.q[qeng].append((waits, fn, key, inc))
        self._mark({key: self.cnt[key]}, rd, wf, wp)

    def barrier(self, engs=ENGS):
        for e in engs:
            waits = []
            for k, v in self.cnt.items():
                if v > 0 and self.waited[e].get(k, 0) < v and not (k == e and e == "pe"):
                    self.waited[e][k] = v
                    waits.append((k, v))
            if waits:
                self.q[e].append((waits, None, None, 0))

    def emit(self):
        nc = self.nc
        keys = list(ENGS) + self.dma_keys
        with contextlib.ExitStack() as st:
            sems = {}
            for i, k in enumerate(keys):
                sems[k] = st.enter_context(nc.semaphore("s%d" % i))
            block = st.enter_context(nc.Block())

            def runq(ename):
                def body(e):
                    for waits, fn, key, inc in self.q[ename]:
                        for k, v in waits:
                            e.wait_ge(sems[k], v)
                        if fn is not None:
                            fn(e).then_inc(sems[key], inc)
                return body
            block.tensor(runq("pe"))
            block.scalar(runq("act"))
            block.vector(runq("dve"))
            block.gpsimd(runq("pool"))
            block.sync(runq("sp"))


class Ring:
    def __init__(self, items):
        self.items = items
        self.i = 0

    def next(self):
        it = self.items[self.i % len(self.items)]
        self.i += 1
        return it


class Arena:
    def __init__(self, t, nbytes, name):
        self.t = t
        self.n = nbytes
        self.name = name
        self.off = 0
        self.gen = 0

    def reset(self):
        self.off = 0
        self.gen += 1

    def carve(self, dtype, shape, n=1, tag=""):
        esz = 4 if dtype == F32 else 2
        per = int(np.prod(shape)) * esz
        out = []
        for i in range(n):
            assert self.off + per <= self.n, (self.name, tag, self.off, per, self.n)
            v = self.t[:, self.off // 2:(self.off + per) // 2]
            if dtype == F32:
                v = v.bitcast(F32)
            if len(shape) == 2:
                v = v.rearrange("p (a b) -> p a b", a=shape[0])
            elif len(shape) == 3:
                v = v.rearrange("p (a b c) -> p a b c", a=shape[0], b=shape[1])
            out.append((v, Buf("%s_%s%d" % (self.name, tag, i))))
            self.off += per
        return out


class StopBuild(Exception):
    pass


def build(nstage, fused, dbg=False, stop=None):
    nc = bass.Bass("TRN2", target_bir_lowering=False)

    def din(name, shape, dt=F32):
        return nc.dram_tensor(name, list(shape), dt, kind="ExternalInput").ap()

    x_d = din("x", [TX, D])
    ctx_d = din("ctx", [TC, D])
    cvec_d = din("cvec", [128, 32])
    pvec_d = din("pvec", [2, 128, NP])
    rope_d = din("rope", [4, 128, TX])
    rot_d = din("rot", [2, 128, 128])
    sel_d = din("sel", [128, 16])
    sgu_g_d = din("sgu_ln_g", [2, 1024])
    sgu_b_d = din("sgu_ln_b", [2, 1024])
    sgu_bs_d = din("sgu_b_s", [2, 1024])
    sgu_ws_d = din("sgu_w_s", [2, 8, 128, 128])
    w_ada_d = din("w_ada", [2, D, 6 * D])
    w_in_d = din("w_in", [2, D, IN_COLS])
    w_uq_d = din("w_uq", [2, 512, 1536])
    w_ukv_d = din("w_ukv", [2, 512, 2048])
    w_br_d = din("w_branch", [2, 4, 1024, D])
    w_out_d = din("w_out", [2, D, D])
    w_ff1_d = din("w_ff1", [2, D, DFF])
    w_ff2_d = din("w_ff2", [2, DFF, D])
    out_d = nc.dram_tensor("out", [TX, D], F32, kind="ExternalOutput").ap()

    kvl_d, kvall_d = [], []
    for l in range(2):
        if fused:
            kvl_d.append(nc.dram_tensor("kvl%d" % l, [ROWS, TX], BF16, kind="Internal").ap())
            kvall_d.append(nc.dram_tensor("kvall%d" % l, [NCORES * ROWS, TX], BF16, kind="Internal").ap())
        else:
            kvl_d.append(nc.dram_tensor("kvl%d" % l, [ROWS, TX], BF16, kind="ExternalOutput").ap())
            kvall_d.append(din("kvall%d" % l, [NCORES * ROWS, TX], BF16))
    ik = "ExternalOutput" if dbg else "Internal"
    ctxk_d = [nc.dram_tensor("ctxk%d" % l, [KROWS, TC], BF16, kind=ik).ap() for l in range(2)]
    ctxv_d = [nc.dram_tensor("ctxv%d" % l, [1280, TC], BF16, kind=ik).ap() for l in range(2)]
    xres_d = nc.dram_tensor("xres", [128, 16 * TX], F32, kind=ik).ap()
    zres_d = nc.dram_tensor("zres", [128, 16 * TC], F32, kind=ik).ap()
    B_kvl = [Buf("kvl0"), Buf("kvl1")]
    B_kvall = [Buf("kvall0"), Buf("kvall1")]
    B_ctxkv = [Buf("ctxkv0"), Buf("ctxkv1")]
    B_xres = Buf("xres")
    B_zres = Buf("zres")
    B_out = Buf("out")
    dbg_d = {}

    st = contextlib.ExitStack()
    E = st.enter_context
    S = Sched(nc)
    if fused:
        S.force_q = "pool"

    a1_t = E(nc.sbuf_tensor("a1", [128, 24576], BF16))
    a2_t = E(nc.sbuf_tensor("a2", [128, 16384], BF16))
    a3_t = E(nc.sbuf_tensor("a3", [128, 32768], BF16))
    wsl_t = [E(nc.sbuf_tensor("wsl%d" % i, [128, 8192], BF16)) for i in range(2)]
    cst_t = E(nc.sbuf_tensor("cst", [128, 1536], F32))
    cb_t = E(nc.sbuf_tensor("cb", [128, 4608], BF16))
    sg_t = E(nc.sbuf_tensor("sg", [128, 3072], F32))
    A1 = Arena(a1_t, 49152, "a1")
    hT = a2_t[:, :].rearrange("p (c t) -> p c t", c=16)
    B_hT = [[Buf("hT%d_%d" % (c, h)) for h in range(2)] for c in range(16)]
    xT = a3_t[:, :].bitcast(F32).rearrange("p (c t) -> p c t", c=16)
    ysv = a3_t[:, :].rearrange("p (i c t) -> p i c t", i=4, c=8)
    B_a3 = [Buf("a3_%d" % i) for i in range(64)]

    def bx(c, h):
        return [B_a3[4 * c + 2 * h], B_a3[4 * c + 2 * h + 1]]

    def bys(i, c, h):
        return [B_a3[(i * 8 + c) * 2 + h]]
    Wv = [wsl_t[i] for i in range(2)]
    B_w = [Buf("w0"), Buf("w1")]
    wctr = [0]
    cst = cst_t
    mod = cst[:, 0:192].rearrange("p (j s) -> p j s", s=2)
    pv = cst[:, 192:192 + NP]
    gs1 = cst[:, 448:480].rearrange("p (c s) -> p c s", s=2)
    gs2 = cst[:, 480:512].rearrange("p (c s) -> p c s", s=2)
    cv = cst[:, 512:544].rearrange("p (c s) -> p c s", s=2)
    selv = cst[:, 544:560]
    ident_f = cst[:, 576:704]
    gbe = cst[:, 704:720].rearrange("p (c e) -> p c e", e=2)
    vle = cst[:, 720:736]
    tmpc = cst[:, 736:800]
    modall = cst[:, 1024:1408].rearrange("p (l j s) -> p l j s", l=2, s=2)
    B_cst = Buf("cst")
    B_mod = Buf("mod")
    B_pv = Buf("pv")
    cb = cb_t
    ropeT = cb[:, 0:4096].rearrange("p (k t) -> p k t", k=4)
    ones_b = cb[:, 4096:4224]
    ident_b = cb[:, 4224:4352]
    rotg = cb[:, 4352:4480]
    rotm = cb[:, 4480:4608]
    B_cb = Buf("cb")
    sgg = sg_t[:, 0:1024]
    sgb = sg_t[:, 1024:2048]
    sgbs = sg_t[:, 2048:3072]
    B_sg = Buf("sg")
    csil = E(nc.sbuf_tensor("csil", [128, 32], BF16))
    B_csil = Buf("csil")
    csv = csil[:, :].rearrange("p (c s) -> p c s", s=2)
    PS = [E(nc.psum_tensor("ps%d" % i, [128, 512], F32)) for i in range(8)]
    B_ps = [Buf("ps%d" % i, excl=True) for i in range(8)]

    def dma_in(q, dst, src, wb, rb=(), part=False, key=None):
        S.dma(q, lambda e: e.dma_start(out=dst, in_=src), reads=list(rb), writes=list(wb), part=part, key=key)

    def wload(src3, kc, ncols):
        i = wctr[0] % 2
        wctr[0] += 1
        dst = Wv[i][:, 0:kc * ncols].rearrange("p (k n) -> p k n", k=kc)
        if kc >= 2:
            h = kc // 2
            S.dma("pool", lambda e: e.dma_start(out=dst[:, 0:h, :], in_=src3[:, 0:h, :]), writes=[B_w[i]], key="d:w%d" % i)
            S.dma("pool", lambda e: e.dma_start(out=dst[:, h:kc, :], in_=src3[:, h:kc, :]), writes=[B_w[i]], key="d:w%d" % i, part=True)
        else:
            S.dma("pool", lambda e: e.dma_start(out=dst, in_=src3), writes=[B_w[i]], key="d:w%d" % i)
        return dst, B_w[i]

    def wsrc(w2d, r0, kc, c0, ncols):
        return w2d[r0:r0 + kc * 128, c0:c0 + ncols].rearrange("(k p) n -> p k n", p=128)

    def mmg(pi, pairs, reads, ncol=512, m=128, c0=0):
        def fn(e):
            n = len(pairs)
            for j, (l, r) in enumerate(pairs):
                ins = e.matmul(PS[pi][0:m, c0:c0 + ncol], l, r, start=(j == 0), stop=(j == n - 1))
            return ins
        S.op("pe", fn, reads=reads, writes=[B_ps[pi]], part=(c0 != 0))

    def act(out, in_, func, reads, writes, bias=None, scale=None, part=False):
        kw = {}
        if bias is not None:
            kw["bias"] = bias
        if scale is not None:
            kw["scale"] = scale
        S.op("act", lambda e: e.activation(out, in_, func, **kw), reads=reads, writes=writes, part=part)

    def tsc(eng, out, in0, s1, s2, op0, op1, reads, writes, part=False):
        if s2 is None:
            S.op(eng, lambda e: e.tensor_scalar(out, in0, s1, None, op0), reads=reads, writes=writes, part=part)
        else:
            S.op(eng, lambda e: e.tensor_scalar(out, in0, s1, s2, op0, op1), reads=reads, writes=writes, part=part)

    def tt(eng, out, a, b, op, reads, writes, part=False):
        S.op(eng, lambda e: e.tensor_tensor(out, a, b, op), reads=reads, writes=writes, part=part)

    def stt(eng, out, in0, sc, in1, op0, op1, reads, writes, part=False):
        S.op(eng, lambda e: e.scalar_tensor_tensor(out, in0, sc, in1, op0, op1), reads=reads, writes=writes, part=part)

    def cp(eng, out, in_, reads, writes, part=False):
        S.op(eng, lambda e: e.tensor_copy(out, in_), reads=reads, writes=writes, part=part)

    def rstd_from(pi, n, inv_n, out, outb, post=None):
        tsc("dve", out, PS[pi][:, 0:n], inv_n, EPS, ALU.mult, ALU.add, [B_ps[pi]], [outb])
        sc = 1.0 if post is None else 1.0 / (post * post)
        act(out, out, AF.Sqrt, [outb], [outb], scale=sc)
        S.op("dve", lambda e: e.reciprocal(out, out), reads=[outb], writes=[outb])

    def setup():
        dma_in("sp", cv.rearrange("p c s -> p (c s)"), cvec_d, [B_cst])
        dma_in("sp", selv, sel_d, [B_cst], part=True)
        dma_in("pool", ropeT, rope_d.rearrange("k p t -> p k t"), [B_cb])
        dma_in("pool", cb[:, 4352:4608].rearrange("p (k m) -> p k m", k=2), rot_d.rearrange("k p m -> p k m"), [B_cb], part=True)
        S.op("pool", lambda e: e.memset(ones_b, 1.0), writes=[B_cb], part=True)
        S.op("pool", lambda e: e.memset(tmpc, 0.0), writes=[B_cst], part=True)

    ident_d = din("ident", [128, 128])

    def setup2():
        dma_in("sp", ident_f, ident_d, [B_cst], part=True)
        dma_in("pool", ident_b, ident_d, [B_cb], part=True)
        ck('s2a')
        t = A1.carve(F32, [32], 1, "sil")[0]
        act(t[0], cv.rearrange("p c s -> p (c s)"), AF.Sigmoid, [B_cst], [t[1]])
        ck('s2b')
        tt("dve", csil[:, :], t[0], cv.rearrange("p c s -> p (c s)"), ALU.mult, [t[1], B_cst], [B_csil])

    def adaln(l):
        wd = w_ada_d[l]
        for cbk in range(24):
            wt, wb = wload(wsrc(wd, 0, 16, cbk * 512, 512), 16, 512)
            for m in range(4):
                j = cbk * 4 + m
                mmg(7, [(wt[:, kc, m * 128:(m + 1) * 128], csv[:, kc, :]) for kc in range(16)], [wb, B_csil], ncol=2, c0=2 * j)
        S.op("dve", lambda e: e.tensor_tensor(modall[:, l], PS[7][:, 0:192].rearrange("p (j s) -> p j s", s=2),
                                              pvl(l)[:, P_BADA:P_BADA + 96].unsqueeze(2).to_broadcast([128, 96, 2]), ALU.add),
             reads=[B_ps[7], B_pv], writes=[B_mod], part=True)

    pvall = E(nc.sbuf_tensor("pvall", [128, 2 * NP], F32))

    def pvl(l):
        return pvall[:, l * NP:(l + 1) * NP]

    def load_pv():
        dma_in("sp", pvall[:, :].rearrange("p (l n) -> p l n", l=2), pvec_d.rearrange("l p n -> p l n"), [B_pv])

    def layer_consts(l):
        for (dst, goff, joff) in ((gs1, P_GMIX, 16), (gs2, P_GFFN, 64)):
            tsc("dve", dst, modall[:, l, joff:joff + 16, :], 1.0, None, ALU.add, None, [B_mod], [B_cst], part=True)
            tt("dve", dst, dst, pvl(l)[:, goff:goff + 16].unsqueeze(2).to_broadcast([128, 16, 2]), ALU.mult, [B_cst, B_pv], [B_cst], part=True)

    class Stream:
        pass
    SX = Stream()
    SX.T, SX.halves, SX.s, SX.rope, SX.name = TX, [(0, 512), (512, 512)], 0, True, "x"
    SCt = Stream()
    SCt.T, SCt.halves, SCt.s, SCt.rope, SCt.name = TC, [(0, 256)], 1, False, "c"

    zT_holder = {}

    def resid(stm):
        if stm is SX:
            return xT, (lambda c, h: bx(c, h))
        return zT_holder["v"], (lambda c, h: [zT_holder["b"][c]])

    def load_rows(stm, src_d):
        R, rb = resid(stm)
        stg = A1.carve(F32, [512], 4, "ldx")
        ring = Ring(stg)
        pr = Ring([0, 1, 2, 3])
        for tt_ in range(stm.T // 128):
            for q4 in range(4):
                sv, sb = ring.next()
                dma_in("sp", sv, src_d[tt_ * 128:(tt_ + 1) * 128, q4 * 512:(q4 + 1) * 512], [sb])
                pi = pr.next()

                def fn(e, sv=sv, pi=pi):
                    for j in range(4):
                        ins = e.transpose(PS[pi][:, j * 128:(j + 1) * 128], sv[:, j * 128:(j + 1) * 128], ident_f)
                    return ins
                S.op("pe", fn, reads=[sb, B_cst], writes=[B_ps[pi]])
                h = (tt_ * 128) // 512
                wb = []
                for j in range(4):
                    wb += rb(q4 * 4 + j, h)
                eng = "dve" if (q4 % 2 == 0) else "act"
                dst = R[:, q4 * 4:(q4 + 1) * 4, tt_ * 128:(tt_ + 1) * 128]
                src = PS[pi][:, :].rearrange("p (j t) -> p j t", j=4)
                if eng == "dve":
                    cp("dve", dst, src, [B_ps[pi]], wb, part=True)
                else:
                    act(dst, src, AF.Copy, [B_ps[pi]], wb, part=True)

    def norm_mod(stm, gsv, sh_j, l):
        R, rb = resid(stm)
        TS = stm.halves[0][1]
        sq = Ring(A1.carve(BF16, [TS], 3, "sq"))
        rs = A1.carve(F32, [TS], 2, "rstd")
        tm = Ring(A1.carve(F32, [TS], 3, "tm"))
        for hi, (t0, tn) in enumerate(stm.halves):
            pi = 6
            for c in range(16):
                qv, qb = sq.next()
                act(qv[:, 0:tn], R[:, c, t0:t0 + tn], AF.Square, rb(c, hi), [qb])
                S.op("pe", lambda e, qv=qv, c=c, tn=tn: e.matmul(PS[pi][:, 0:tn], ones_b, qv[:, 0:tn], start=(c == 0), stop=(c == 15)),
                     reads=[qb, B_cb], writes=[B_ps[pi]], part=(c != 0))
            rv, rbuf = rs[hi % 2]
            rstd_from(pi, tn, 1.0 / D, rv[:, 0:tn], rbuf)
            for c in range(16):
                tv, tb = tm.next()
                stt("dve", tv[:, 0:tn], R[:, c, t0:t0 + tn], gsv[:, c, stm.s:stm.s + 1], rv[:, 0:tn], ALU.mult, ALU.mult,
                    rb(c, hi) + [B_cst, rbuf], [tb])
                act(hT[:, c, t0:t0 + tn], tv[:, 0:tn], AF.Identity, [tb, B_mod], [B_hT[c][hi]],
                    bias=modall[:, l, sh_j + c, stm.s:stm.s + 1])

    def hT_reads(hi):
        return [B_hT[c][hi] for c in range(16)]

    def head_norm_rope(stm, pi_raw, n, gcol, cs_idx, t0, out, outb, post, pools, pi_ss=6, pi_rot=5):
        sqr, qgr, f32r = pools
        qv, qb = sqr.next()
        act(qv[:, 0:n], PS[pi_raw][:, 0:n], AF.Square, [B_ps[pi_raw]], [qb])
        gv, gb = qgr.next()
        tsc("dve", gv[:, 0:n], PS[pi_raw][:, 0:n], gcol, None, ALU.mult, None, [B_ps[pi_raw], B_pv], [gb])
        S.op("pe", lambda e: e.matmul(PS[pi_ss][:, 0:n], ones_b, qv[:, 0:n], start=True, stop=True), reads=[qb, B_cb], writes=[B_ps[pi_ss]])
        rv, rbuf = f32r.next()
        rstd_from(pi_ss, n, 1.0 / 128, rv[:, 0:n], rbuf, post=post)
        if stm.rope:
            S.op("pe", lambda e: e.matmul(PS[pi_rot][:, 0:n], rotg, gv[:, 0:n], start=True, stop=True), reads=[gb, B_cb], writes=[B_ps[pi_rot]])
            t1, b1 = f32r.next()
            tt("pool", t1[:, 0:n], gv[:, 0:n], ropeT[:, cs_idx, t0:t0 + n], ALU.mult, [gb, B_cb], [b1])
            t2, b2 = f32r.next()
            tt("dve", t2[:, 0:n], PS[pi_rot][:, 0:n], ropeT[:, cs_idx + 1, t0:t0 + n], ALU.mult, [B_ps[pi_rot], B_cb], [b2])
            tt("pool", t1[:, 0:n], t1[:, 0:n], t2[:, 0:n], ALU.add, [b1, b2], [b1])
            tt("dve", out, t1[:, 0:n], rv[:, 0:n], ALU.mult, [b1, rbuf], outb)
        else:
            tt("dve", out, gv[:, 0:n], rv[:, 0:n], ALU.mult, [gb, rbuf], outb)

    def mla_rope(stm, pi_raw, m, n, t0, out, outb, post, pools, pi_rot=5):
        sqr, qgr, f32r = pools
        gv, gb = qgr.next()
        if post is None:
            act(gv[0:m, 0:n], PS[pi_raw][0:m, 0:n], AF.Copy, [B_ps[pi_raw]], [gb])
        else:
            act(gv[0:m, 0:n], PS[pi_raw][0:m, 0:n], AF.Copy, [B_ps[pi_raw]], [gb], scale=post)
        if not stm.rope:
            cp("dve", out, gv[0:m, 0:n], [gb], outb)
            return
        S.op("pe", lambda e: e.matmul(PS[pi_rot][0:m, 0:n], rotm[0:m, 0:m], gv[0:m, 0:n], start=True, stop=True),
             reads=[gb, B_cb], writes=[B_ps[pi_rot]])
        t1, b1 = f32r.next()
        tt("pool", t1[0:m, 0:n], gv[0:m, 0:n], ropeT[0:m, 2, t0:t0 + n], ALU.mult, [gb, B_cb], [b1])
        t2, b2 = f32r.next()
        tt("dve", t2[0:m, 0:n], PS[pi_rot][0:m, 0:n], ropeT[0:m, 3, t0:t0 + n], ALU.mult, [B_ps[pi_rot], B_cb], [b2])
        tt("dve", out, t1[0:m, 0:n], t2[0:m, 0:n], ALU.add, [b1, b2], outb)

    def rms512(stm, l, wt, wb, col0, gcol0, out_fn, pools_f, hi, t0, tn):
        rawr, sqr, rsr = pools_f
        raws = []
        for c in range(4):
            pi = c % 4
            mmg(pi, [(wt[:, kc, col0 + c * 128:col0 + (c + 1) * 128], hT[:, kc, t0:t0 + tn]) for kc in range(16)],
                [wb] + hT_reads(hi), ncol=tn)
            rv, rb_ = rawr.next()
            cp("dve", rv[:, 0:tn], PS[pi][:, 0:tn], [B_ps[pi]], [rb_])
            qv, qb = sqr.next()
            act(qv[:, 0:tn], PS[pi][:, 0:tn], AF.Square, [B_ps[pi]], [qb])
            S.op("pe", lambda e, qv=qv, c=c: e.matmul(PS[6][:, 0:tn], ones_b, qv[:, 0:tn], start=(c == 0), stop=(c == 3)),
                 reads=[qb, B_cb], writes=[B_ps[6]], part=(c != 0))
            raws.append((rv, rb_))
        sv, sb = rsr.next()
        rstd_from(6, tn, 1.0 / 512, sv[:, 0:tn], sb)
        for c in range(4):
            ov, ob = out_fn(c)
            stt("dve", ov, raws[c][0][:, 0:tn], pvl(l)[:, gcol0 + c:gcol0 + c + 1], sv[:, 0:tn], ALU.mult, ALU.mult,
                [raws[c][1], B_pv, sb], ob)

    cq_holder = {}

    def kv_targets(stm, l):
        if stm is SX:
            return kvl_d[l], kvl_d[l], VROW0, B_kvl[l], B_kvl[l]
        return ctxk_d[l], ctxv_d[l], 0, B_ctxkv[l], B_ctxkv[l]

    def phase_kv(stm, l):
        T = stm.T
        NT = T // 128
        kd, vd, vr0, kbuf, vbuf = kv_targets(stm, l)
        wd = w_in_d[l]
        TS = stm.halves[0][1]
        sqr = Ring(A1.carve(BF16, [TS], 3, "sq"))
        qgr = Ring(A1.carve(BF16, [TS], 2, "qg"))
        f32r = Ring(A1.carve(F32, [TS], 4, "f32"))
        outr = Ring(A1.carve(BF16, [TS], 3, "out"))
        rawr = Ring(A1.carve(F32, [TS], 4, "raw"))
        rsr = Ring(A1.carve(F32, [TS], 1, "rs"))
        ckv = A1.carve(BF16, [4, T], 1, "ckv")[0]
        vst = Ring(A1.carve(BF16, [1024], 2, "vst"))
        pools = (sqr, qgr, f32r)
        wt, wb = wload(wsrc(wd, 0, 16, 0, 512), 16, 512)
        for hi, (t0, tn) in enumerate(stm.halves):
            for j in range(2):
                mmg(0 + j, [(wt[:, kc, j * 128:(j + 1) * 128], hT[:, kc, t0:t0 + tn]) for kc in range(16)], [wb] + hT_reads(hi), ncol=tn)
                ov, ob = outr.next()
                head_norm_rope(stm, 0 + j, tn, pvl(l)[:, P_KG:P_KG + 1], 0, t0, ov[:, 0:tn], [ob], None, pools)
                S.dma("sp", lambda e, ov=ov, j=j, t0=t0, tn=tn: e.dma_start(out=kd[j * 128:(j + 1) * 128, t0:t0 + tn], in_=ov[:, 0:tn]),
                      reads=[ob], writes=[kbuf], key="d:st_" + ob.name, part=True)
        ck('kv_a')
        for tt_ in range(NT):
            hi = (tt_ * 128) // 512
            pi = 2 + (tt_ % 2)
            mmg(pi, [(hT[:, kc, tt_ * 128:(tt_ + 1) * 128], wt[:, kc, 256:512]) for kc in range(16)], [wb] + hT_reads(hi), ncol=256)
            sv, sb = vst.next()
            if tt_ % 2 == 0:
                cp("dve", sv[:, 0:256], PS[pi][:, 0:256], [B_ps[pi]], [sb])
            else:
                act(sv[:, 0:256], PS[pi][:, 0:256], AF.Copy, [B_ps[pi]], [sb])
            for j in range(2):
                S.dma("sp", lambda e, sv=sv, j=j, tt_=tt_: e.dma_start(out=vd[vr0 + j * 128:vr0 + (j + 1) * 128, tt_ * 128:(tt_ + 1) * 128],
                                                                      in_=sv[:, j * 128:(j + 1) * 128]),
                      reads=[sb], writes=[vbuf], key="d:st_" + sb.name, part=True)
        ck('kv_b')
        wt, wb = wload(wsrc(wd, 0, 16, 512, 512), 16, 512)
        B_ckv = [[Buf("ckv%d_%d" % (c, h)) for h in range(2)] for c in range(4)]
        for hi, (t0, tn) in enumerate(stm.halves):
            rms512(stm, l, wt, wb, 0, P_MKG, lambda c, hi=hi, t0=t0, tn=tn: (ckv[0][:, c, t0:t0 + tn], [B_ckv[c][hi]]),
                   (rawr, sqr, rsr), hi, t0, tn)
        ck('kv_c')
        wk = w_ukv_d[l].rearrange("(k p) (h t d) -> p k h t d", p=128, h=8, t=2)
        i = wctr[0] % 2
        wctr[0] += 1
        wuk = Wv[i][:, 0:8192].rearrange("p (k t h d) -> p k t h d", k=4, t=2, h=8)
        for t2 in range(2):
            for kc in range(4):
                S.dma("pool", lambda e, t2=t2, kc=kc: e.dma_start(out=wuk[:, kc, t2], in_=wk[:, kc, :, t2, :]), writes=[B_w[i]],
                      key="d:w%d" % i, part=not (t2 == 0 and kc == 0))
        wb = B_w[i]
        for hi, (t0, tn) in enumerate(stm.halves):
            for h in range(8):
                pi = h % 2
                mmg(pi, [(wuk[:, kc, 0, h, :], ckv[0][:, kc, t0:t0 + tn]) for kc in range(4)], [wb] + [B_ckv[c][hi] for c in range(4)], ncol=tn)
                ov, ob = outr.next()
                if h % 2 == 0:
                    cp("dve", ov[:, 0:tn], PS[pi][:, 0:tn], [B_ps[pi]], [ob])
                else:
                    act(ov[:, 0:tn], PS[pi][:, 0:tn], AF.Copy, [B_ps[pi]], [ob])
                S.dma("sp", lambda e, ov=ov, h=h, t0=t0, tn=tn: e.dma_start(out=kd[256 + h * 128:256 + (h + 1) * 128, t0:t0 + tn], in_=ov[:, 0:tn]),
                      reads=[ob], writes=[kbuf], key="d:st_" + ob.name, part=True)
        ck('kv_d')
        for tt_ in range(NT):
            hi = (tt_ * 128) // 512
            sv, sb = vst.next()
            for g in range(2):
                pi = 2 + g
                mmg(pi, [(ckv[0][:, kc, tt_ * 128:(tt_ + 1) * 128], wuk[:, kc, 1, g * 4:(g + 1) * 4, :].rearrange("p h d -> p (h d)")) for kc in range(4)],
                    [wb] + [B_ckv[c][hi] for c in range(4)], ncol=512)
                if g == 0:
                    cp("dve", sv[:, 0:512], PS[pi][:, :], [B_ps[pi]], [sb])
                else:
                    act(sv[:, 512:1024], PS[pi][:, :], AF.Copy, [B_ps[pi]], [sb], part=True)
            S.dma("sp", lambda e, sv=sv, tt_=tt_: e.dma_start(
                out=vd[vr0 + 256:vr0 + 1280, tt_ * 128:(tt_ + 1) * 128].rearrange("(h p) d -> p h d", p=128),
                in_=sv[:, :].rearrange("p (h d) -> p h d", h=8)), reads=[sb], writes=[vbuf], key="d:st_" + sb.name, part=True)
        ck('kv_e')
        wt, wb = wload(wsrc(wd, 0, 16, C_KR, 64), 16, 64)
        for hi, (t0, tn) in enumerate(stm.halves):
            mmg(4, [(wt[:, kc, 0:64], hT[:, kc, t0:t0 + tn]) for kc in range(16)], [wb] + hT_reads(hi), ncol=tn, m=64)
            ov, ob = outr.next()
            mla_rope(stm, 4, 64, tn, t0, ov[0:64, 0:tn], [ob], None, pools)
            for dup in range(2):
                S.dma("sp", lambda e, ov=ov, dup=dup, t0=t0, tn=tn: e.dma_start(out=kd[1280 + dup * 64:1344 + dup * 64, t0:t0 + tn], in_=ov[0:64, 0:tn]),
                      reads=[ob], writes=[kbuf], key="d:st_" + ob.name, part=True)

    def attention(stm, l):
        T = stm.T
        TS = stm.halves[0][1]
        wd = w_in_d[l]
        carve_cq(T)
        cq = cq_holder["v"]
        sqr = Ring(A1.carve(BF16, [TS], 2, "sq"))
        qgr = Ring(A1.carve(BF16, [TS], 2, "qg"))
        f32r = Ring(A1.carve(F32, [TS], 5, "f32"))
        qbuf = Ring(A1.carve(BF16, [T], 2, "q"))
        qrbuf = Ring(A1.carve(BF16, [T], 2, "qr"))
        kblk = Ring(A1.carve(BF16, [1024], 2, "k"))
        krblk = Ring(A1.carve(BF16, [1024], 2, "kr"))
        vblk = Ring(A1.carve(BF16, [8, 128], 2, "v"))
        ptr = Ring(A1.carve(BF16, [TS], 4, "pt"))
        rcr = f32r
        pools = (sqr, qgr, f32r)
        wt, wb = wload(wsrc(wd, 0, 16, C_CQ, 512), 16, 512)
        for hi, (t0, tn) in enumerate(stm.halves):
            rms512(stm, l, wt, wb, 0, P_MQG, lambda c, hi=hi, t0=t0, tn=tn: (cq[0][:, c, t0:t0 + tn], [cq_holder["b"][c][hi]]),
                   (f32r, sqr, f32r), hi, t0, tn)
        sring = Ring([0, 1, 2])
        if stm is SX:
            blocks = [(ctxk_d[l], ctxv_d[l], 0, 0, TC, B_ctxkv[l])]
            for r in range(NCORES):
                blocks.append((kvall_d[l], kvall_d[l], r * ROWS, r * ROWS + VROW0, TX, B_kvall[l]))
        else:
            blocks = [(ctxk_d[l], ctxv_d[l], 0, 0, TC, B_ctxkv[l])]
        nkt_total = sum(b[4] // 128 for b in blocks)
        wq = [None, None]
        wuq = [None]
        qr_pair = [None]
        for head in range(16):
            mla = head >= 8
            h = head % 8
            qv, qb = qbuf.next()
            if not mla:
                if h % 4 == 0:
                    wq = wload(wsrc(wd, 0, 16, C_QB + (h // 4) * 512, 512), 16, 512)
                wt, wb = wq
                for hi, (t0, tn) in enumerate(stm.halves):
                    mmg(7, [(wt[:, kc, (h % 4) * 128:(h % 4 + 1) * 128], hT[:, kc, t0:t0 + tn]) for kc in range(16)], [wb] + hT_reads(hi), ncol=tn)
                    head_norm_rope(stm, 7, tn, pvl(l)[:, P_QG:P_QG + 1], 0, t0, qv[:, t0:t0 + tn], [qb], 128.0 ** -0.5, pools)
            else:
                if h == 0:
                    wu = w_uq_d[l].rearrange("(k p) (h d) -> p k h d", p=128, h=8)
                    i = wctr[0] % 2
                    wctr[0] += 1
                    wun = Wv[i][:, 0:4096].rearrange("p (k h d) -> p k h d", k=4, h=8)
                    wur = Wv[i][:, 4096:6144].rearrange("p (k h d) -> p k h d", k=4, h=8)
                    for kc in range(4):
                        S.dma("pool", lambda e, kc=kc: e.dma_start(out=wun[:, kc], in_=wu[:, kc, :, 0:128]), writes=[B_w[i]], key="d:w%d" % i, part=(kc != 0))
                        S.dma("pool", lambda e, kc=kc: e.dma_start(out=wur[:, kc], in_=wu[:, kc, :, 128:192]), writes=[B_w[i]], key="d:w%d" % i, part=True)
                    wuq[0] = (wun, wur, B_w[i])
                wun, wur, wb = wuq[0]
                cqr = [cq_holder["b"][c] for c in range(4)]
                for hi, (t0, tn) in enumerate(stm.halves):
                    mmg(7, [(wun[:, kc, h, :], cq[0][:, kc, t0:t0 + tn]) for kc in range(4)], [wb] + [cqr[c][hi] for c in range(4)], ncol=tn)
                    act(qv[:, t0:t0 + tn], PS[7][:, 0:tn], AF.Copy, [B_ps[7]], [qb], scale=192.0 ** -0.5, part=(hi != 0))
                if h % 2 == 0:
                    qr_pair[0] = qrbuf.next()
                    rv_, rb_ = qr_pair[0]
                    for hi, (t0, tn) in enumerate(stm.halves):
                        mmg(7, [(wur[:, kc, h:h + 2, :].rearrange("p h d -> p (h d)"), cq[0][:, kc, t0:t0 + tn]) for kc in range(4)],
                            [wb] + [cqr[c][hi] for c in range(4)], ncol=tn)
                        mla_rope(stm, 7, 128, tn, t0, rv_[:, t0:t0 + tn], [rb_], 192.0 ** -0.5, pools)
            kti = 0
            LA = 2
            pend = []

            def flush(nkeep):
                while len(pend) > nkeep:
                    (vv_, vb_, kt_, pv_, pb_, hi_, tn_, first_, last_) = pend.pop(0)
                    S.op("pe", lambda e, vv_=vv_, kt_=kt_, pv_=pv_, hi_=hi_, tn_=tn_, first_=first_, last_=last_:
                         e.matmul(PS[3 + hi_][:, 0:tn_], vv_[:, kt_, :], pv_[:, 0:tn_], start=first_, stop=last_),
                         reads=[vb_, pb_], writes=[B_ps[3 + hi_]], part=not first_)
                    S.op("pe", lambda e, pv_=pv_, hi_=hi_, tn_=tn_, first_=first_, last_=last_:
                         e.matmul(PS[5 + hi_][:, 0:tn_], ones_b, pv_[:, 0:tn_], start=first_, stop=last_),
                         reads=[pb_, B_cb], writes=[B_ps[5 + hi_]], part=not first_)
            for (kdr, vdr, krow0, vrow0, nk, kvb) in blocks:
                nt = nk // 128
                kv_, kb_ = kblk.next()
                if not mla:
                    r0 = krow0 + (h // 4) * 128
                    vr = vrow0 + (h // 4) * 128
                else:
                    r0 = krow0 + 256 + h * 128
                    vr = vrow0 + 256 + h * 128
                    krv, krb = krblk.next()
                    dma_in("sp", krv[:, 0:nk], kdr[krow0 + 1280:krow0 + 1408, 0:nk], [krb], rb=[kvb])
                dma_in("sp", kv_[:, 0:nk], kdr[r0:r0 + 128, 0:nk], [kb_], rb=[kvb])
                vv, vb = vblk.next()
                dma_in("sp", vv[:, 0:nt, :], vdr[vr:vr + 128, 0:nk].rearrange("p (t d) -> p t d", d=128), [vb], rb=[kvb])
                for kt in range(nt):
                    for hi, (t0, tn) in enumerate(stm.halves):
                        si = sring.next()
                        pairs = [(kv_[:, kt * 128:(kt + 1) * 128], qv[:, t0:t0 + tn])]
                        rd = [kb_, qb]
                        if mla:
                            o = (h % 2) * 64
                            pairs.append((krv[o:o + 64, kt * 128:(kt + 1) * 128], qr_pair[0][0][o:o + 64, t0:t0 + tn]))
                            rd += [krb, qr_pair[0][1]]
                        mmg(si, pairs, rd, ncol=tn)
                        pv_, pb_ = ptr.next()
                        act(pv_[:, 0:tn], PS[si][:, 0:tn], AF.Exp, [B_ps[si]], [pb_])
                        first, last = (kti == 0), (kti == nkt_total - 1)
                        pend.append((vv, vb, kt, pv_, pb_, hi, tn, first, last))
                        flush(LA)
                    kti += 1
            flush(0)
            br = 2 if mla else 1
            for hi, (t0, tn) in enumerate(stm.halves):
                rv, rb_ = rcr.next()
                S.op("dve", lambda e, rv=rv, hi=hi, tn=tn: e.reciprocal(rv[:, 0:tn], PS[5 + hi][:, 0:tn]), reads=[B_ps[5 + hi]], writes=[rb_])
                tt("dve", ysv[:, br, h, t0:t0 + tn], PS[3 + hi][:, 0:tn], rv[:, 0:tn], ALU.mult, [B_ps[3 + hi], rb_], bys(br, h, hi))

    def conv(stm, l):
        T = stm.T
        TS = stm.halves[0][1]
        wd = w_in_d[l]
        vt = A1.carve(F32, [T + 2], 2, "cv")
        gbr = Ring(A1.carve(F32, [TS], 4, "gb"))
        acc = Ring(A1.carve(F32, [TS], 3, "acc"))
        cw = pvl(l)[:, P_CONV:P_CONV + 24].rearrange("p (t c) -> p t c", t=3)
        for c in range(8):
            i = wctr[0] % 2
            wctr[0] += 1
            w3 = Wv[i][:, 0:6144].rearrange("p (k s n) -> p k s n", k=16, s=3)
            for s3 in range(3):
                S.dma("pool", lambda e, s3=s3, c=c, w3=w3: e.dma_start(out=w3[:, :, s3, :], in_=wsrc(wd, 0, 16, C_A + s3 * 1024 + c * 128, 128)),
                      writes=[B_w[i]], key="d:w%d" % i, part=(s3 != 0))
            wb = B_w[i]
            vv, vb = vt[c % 2]
            S.op("pool", lambda e, vv=vv: e.memset(vv[:, 0:1], 0.0), writes=[vb])
            S.op("pool", lambda e, vv=vv, T=T: e.memset(vv[:, T + 1:T + 2], 0.0), writes=[vb], part=True)
            gts = []
            for hi, (t0, tn) in enumerate(stm.halves):
                for s3, pi in ((1, 0), (2, 1), (0, 2)):
                    mmg(pi, [(w3[:, kc, s3, :], hT[:, kc, t0:t0 + tn]) for kc in range(16)], [wb] + hT_reads(hi), ncol=tn)
                tv, tb = gbr.next()
                act(tv[:, 0:tn], PS[0][:, 0:tn], AF.Copy, [B_ps[0]], [tb])
                tt("dve", vv[:, 1 + t0:1 + t0 + tn], PS[1][:, 0:tn], tv[:, 0:tn], ALU.mult, [B_ps[1], tb], [vb], part=True)
                gv, gb = gbr.next()
                act(gv[:, 0:tn], PS[2][:, 0:tn], AF.Copy, [B_ps[2]], [gb])
                gts.append((gv, gb))
            if stm is SX:
                g0, g1 = gts[0], gts[-1]
                ln = stm.halves[-1][1]
                cp("dve", gbe[:, c, 0:1], g0[0][:, 0:1], [g0[1]], [B_cst], part=True)
                cp("dve", gbe[:, c, 1:2], g1[0][:, ln - 1:ln], [g1[1]], [B_cst], part=True)
                cp("dve", hal[:, c, 0:1], vv[:, 1:2], [vb], [B_hal], part=True)
                cp("dve", hal[:, c, 1:2], vv[:, T:T + 1], [vb], [B_hal], part=True)
            for hi, (t0, tn) in enumerate(stm.halves):
                av, ab = acc.next()
                tsc("dve", av[:, 0:tn], vv[:, t0:t0 + tn], cw[:, 0, c:c + 1], None, ALU.mult, None, [vb, B_pv], [ab])
                stt("dve", av[:, 0:tn], vv[:, t0 + 1:t0 + 1 + tn], cw[:, 1, c:c + 1], av[:, 0:tn], ALU.mult, ALU.add, [vb, B_pv, ab], [ab])
                stt("dve", av[:, 0:tn], vv[:, t0 + 2:t0 + 2 + tn], cw[:, 2, c:c + 1], av[:, 0:tn], ALU.mult, ALU.add, [vb, B_pv, ab], [ab])
                tt("pool", ysv[:, 0, c, t0:t0 + tn], av[:, 0:tn], gts[hi][0][:, 0:tn], ALU.mult, [ab, gts[hi][1]], bys(0, c, hi))

    hal_t = E(nc.sbuf_tensor("hal", [128, 16], BF16))
    hal = hal_t[:, :].rearrange("p (c e) -> p c e", e=2)
    B_hal = Buf("hal")
    halall_t = E(nc.sbuf_tensor("halall", [128, 128], BF16))

    def halo_out(l):
        for r2 in range(2):
            S.dma("sp", lambda e, r2=r2: e.dma_start(out=kvl_d[l][HROW0 + r2, :].rearrange("(p f) -> p f", p=128), in_=hal_t[:, r2 * 8:(r2 + 1) * 8]),
                  reads=[B_hal], writes=[B_kvl[l]], key="d:halo", part=True)

    def halo_fix(l):
        ha = halall_t[:, :].rearrange("p (r f) -> p r f", r=8)
        Bh = Buf("halall")
        for r2 in range(2):
            src = kvall_d[l].rearrange("(r w) t -> r w t", r=NCORES)[:, HROW0 + r2, :].rearrange("r (p f) -> p r f", p=128)
            S.dma("sp", lambda e, r2=r2, src=src: e.dma_start(out=ha[:, :, r2 * 8:(r2 + 1) * 8], in_=src), reads=[B_kvall[l]], writes=[Bh], part=(r2 != 0))
        ha4 = halall_t[:, :].rearrange("p (r c e) -> p r c e", r=8, e=2)
        vl = vle.rearrange("p (e c) -> p e c", e=2)
        for e_ in range(2):
            for r in range(NCORES):
                src_r = ha4[:, r, :, 1 - e_]
                if r == 0:
                    tsc("dve", vl[:, e_, :], src_r, selv[:, e_ * 8 + r:e_ * 8 + r + 1], None, ALU.mult, None, [Bh, B_cst], [B_cst], part=True)
                else:
                    stt("dve", vl[:, e_, :], src_r, selv[:, e_ * 8 + r:e_ * 8 + r + 1], vl[:, e_, :], ALU.mult, ALU.add, [Bh, B_cst], [B_cst], part=True)
        cw = pvl(l)[:, P_CONV:P_CONV + 24].rearrange("p (t c) -> p t c", t=3)
        for e_, tap, tok in ((0, 0, 0), (1, 2, TX - 1)):
            tv = tmpc[:, e_ * 8:(e_ + 1) * 8]
            tt("dve", tv, vl[:, e_, :], cw[:, tap, :], ALU.mult, [B_cst, B_pv], [B_cst], part=True)
            tt("dve", tv, tv, gbe[:, :, e_], ALU.mult, [B_cst], [B_cst], part=True)
            yb = []
            for c in range(8):
                yb += bys(0, c, 0 if tok == 0 else 1)
            tt("dve", ysv[:, 0, :, tok], ysv[:, 0, :, tok], tv, ALU.add, yb + [B_cst], yb, part=True)

    def sgu_consts(l):
        dma_in("sp", sgg, sgu_g_d[l:l + 1, :].partition_broadcast(128).rearrange("p o n -> p (o n)"), [B_sg])
        dma_in("sp", sgb, sgu_b_d[l:l + 1, :].partition_broadcast(128).rearrange("p o n -> p (o n)"), [B_sg], part=True)
        dma_in("sp", sgbs, sgu_bs_d[l:l + 1, :].partition_broadcast(128).rearrange("p o n -> p (o n)"), [B_sg], part=True)

    def gelu_from_psum(pi, n, out, outb, r1, r2, eng2="pool"):
        xv, xb = r1.next()
        act(xv[:, 0:n], PS[pi][:, 0:n], AF.Copy, [B_ps[pi]], [xb])
        uv, ub = r2.next()
        tt(eng2, uv[:, 0:n], xv[:, 0:n], xv[:, 0:n], ALU.mult, [xb], [ub])
        tsc("dve", uv[:, 0:n], uv[:, 0:n], 0.044715, 1.0, ALU.mult, ALU.add, [ub], [ub])
        tt("dve", uv[:, 0:n], uv[:, 0:n], xv[:, 0:n], ALU.mult, [ub, xb], [ub])
        act(uv[:, 0:n], uv[:, 0:n], AF.Sigmoid, [ub], [ub], scale=1.5957691216057308)
        tt("dve", out, uv[:, 0:n], xv[:, 0:n], ALU.mult, [ub, xb], outb)

    def sgu(stm, l):
        T = stm.T
        TS = stm.halves[0][1]
        NT = T // 128
        wd = w_in_d[l]
        vn_t = A1.carve(BF16, [NT, 1024], 1, "vn")[0]
        B_vn = [Buf("vn%d" % t) for t in range(NT)]
        r1 = Ring(A1.carve(F32, [512], 3, "g1"))
        r2 = Ring(A1.carve(F32, [512], 2, "g2"))
        vf = Ring(A1.carve(F32, [1024], 2, "vf"))
        wsT_t = A1.carve(BF16, [8, 128], 1, "wsT")[0]
        ur = Ring(A1.carve(F32, [TS], 2, "u"))
        stv = cst[:, 800:832].rearrange("p (k n) -> p k n", k=4)
        st6 = Ring([(stv[:, k, :], Buf("sst%d" % k)) for k in range(4)])
        wsn, wsnb = vf.next()
        wsn3 = wsn[:, :].rearrange("p (g q) -> p g q", g=8)
        dma_in("sp", wsn3, sgu_ws_d[l].rearrange("g p q -> p g q"), [wsnb])
        for g2 in range(2):
            def fn(e, g2=g2):
                for j in range(4):
                    ins = e.transpose(PS[7][:, j * 128:(j + 1) * 128], wsn3[:, g2 * 4 + j, :], ident_f)
                return ins
            S.op("pe", fn, reads=[wsnb, B_cst], writes=[B_ps[7]])
            cp("dve", wsT_t[0][:, g2 * 4:(g2 + 1) * 4, :], PS[7][:, :].rearrange("p (j q) -> p j q", j=4), [B_ps[7]], [wsT_t[1]], part=(g2 != 0))
        wv0 = wload(wsrc(wd, 0, 16, C_D + 1024, 512), 16, 512)
        wv1 = wload(wsrc(wd, 0, 16, C_D + 1536, 512), 16, 512)
        for tt_ in range(NT):
            hi = (tt_ * 128) // 512
            fv, fb = vf.next()
            for g, wvx in enumerate((wv0, wv1)):
                pi = g
                mmg(pi, [(hT[:, kc, tt_ * 128:(tt_ + 1) * 128], wvx[0][:, kc, :]) for kc in range(16)], [wvx[1]] + hT_reads(hi), ncol=512)
                gelu_from_psum(pi, 512, fv[:, g * 512:(g + 1) * 512], [fb], r1, r2)
            sv, sb = st6.next()
            S.op("dve", lambda e, fv=fv, sv=sv: e.reduce_sum(sv[:, 0:1], fv[:, :], mybir.AxisListType.X), reads=[fb], writes=[sb])
            tsc("dve", sv[:, 0:1], sv[:, 0:1], -1.0 / 1024, None, ALU.mult, None, [sb], [sb])
            tsc("dve", fv[:, :], fv[:, :], sv[:, 0:1], None, ALU.add, None, [fb, sb], [fb])
            qv, qb = r1.next()
            tt("dve", qv[:, 0:512], fv[:, 0:512], fv[:, 0:512], ALU.mult, [fb], [qb])
            q2, qb2 = r1.next()
            tt("pool", q2[:, 0:512], fv[:, 512:1024], fv[:, 512:1024], ALU.mult, [fb], [qb2])
            tt("dve", qv[:, 0:512], qv[:, 0:512], q2[:, 0:512], ALU.add, [qb, qb2], [qb])
            S.op("dve", lambda e, qv=qv, sv=sv: e.reduce_sum(sv[:, 1:2], qv[:, 0:512], mybir.AxisListType.X), reads=[qb], writes=[sb], part=True)
            tsc("dve", sv[:, 1:2], sv[:, 1:2], 1.0 / 1024, EPS, ALU.mult, ALU.add, [sb], [sb])
            act(sv[:, 1:2], sv[:, 1:2], AF.Sqrt, [sb], [sb])
            S.op("dve", lambda e, sv=sv: e.reciprocal(sv[:, 1:2], sv[:, 1:2]), reads=[sb], writes=[sb])
            stt("dve", fv[:, :], fv[:, :], sv[:, 1:2], sgg, ALU.mult, ALU.mult, [fb, sb, B_sg], [fb])
            tt("pool", vn_t[0][:, tt_, :], fv[:, :], sgb, ALU.add, [fb, B_sg], [B_vn[tt_]])
        for g2 in range(2):
            wu_ = wload(wsrc(wd, 0, 16, C_D + g2 * 512, 512), 16, 512)
            for m in range(4):
                g = g2 * 4 + m
                for hi, (t0, tn) in enumerate(stm.halves):
                    mmg(0, [(wu_[0][:, kc, m * 128:(m + 1) * 128], hT[:, kc, t0:t0 + tn]) for kc in range(16)], [wu_[1]] + hT_reads(hi), ncol=tn)
                    uv, ub = ur.next()
                    gelu_from_psum(0, tn, uv[:, 0:tn], [ub], r1, r2)
                    ntl = tn // 128

                    def fn(e, g=g, t0=t0, ntl=ntl):
                        for j in range(ntl):
                            tt_ = t0 // 128 + j
                            ins = e.matmul(PS[1][:, j * 128:(j + 1) * 128], vn_t[0][:, tt_, g * 128:(g + 1) * 128], wsT_t[0][:, g, :], start=True, stop=True)
                        return ins
                    S.op("pe", fn, reads=[B_vn[t0 // 128 + j] for j in range(ntl)] + [wsT_t[1]], writes=[B_ps[1]])
                    mv, mb = r2.next()
                    tt("dve", mv[:, 0:tn].rearrange("p (j q) -> p j q", q=128), PS[1][:, 0:tn].rearrange("p (j q) -> p j q", q=128),
                       sgbs[:, g * 128:(g + 1) * 128].unsqueeze(1).to_broadcast([128, ntl, 128]), ALU.add, [B_ps[1], B_sg], [mb])
                    tt("pool", ysv[:, 3, g, t0:t0 + tn], mv[:, 0:tn], uv[:, 0:tn], ALU.mult, [mb, ub], bys(3, g, hi))

    def merge_out(stm, l, mg_v, mg_b, xold_fn):
        T = stm.T
        wd = w_in_d[l]
        gr = Ring(P2["gate"])
        ar = Ring(P2["acc"])
        R, rb = resid(stm)
        for dq in range(8):
            accs = {}
            for i4 in range(4):
                wg = wload(wsrc(wd, 0, 16, C_G + i4 * D + dq * 256, 256), 16, 256)
                wbr = wload(wsrc(w_br_d[l, i4], 0, 8, dq * 256, 256), 8, 256)
                for m in range(2):
                    d = dq * 2 + m
                    for hi, (t0, tn) in enumerate(stm.halves):
                        mmg(0 + hi, [(wg[0][:, kc, m * 128:(m + 1) * 128], hT[:, kc, t0:t0 + tn]) for kc in range(16)], [wg[1]] + hT_reads(hi), ncol=tn)
                        gv, gb = gr.next()
                        act(gv[:, 0:tn], PS[0 + hi][:, 0:tn], AF.Sigmoid, [B_ps[0 + hi], B_pv], [gb],
                            bias=pvl(l)[:, P_BGATE + i4 * 16 + d:P_BGATE + i4 * 16 + d + 1])
                        mmg(2 + hi, [(wbr[0][:, kc, m * 128:(m + 1) * 128], ysv[:, i4, kc, t0:t0 + tn]) for kc in range(8)],
                            [wbr[1]] + [bys(i4, kc, hi)[0] for kc in range(8)], ncol=tn)
                        if i4 == 0:
                            accs[(m, hi)] = ar.next()
                            av, ab = accs[(m, hi)]
                            tt("dve", av[:, 0:tn], PS[2 + hi][:, 0:tn], gv[:, 0:tn], ALU.mult, [B_ps[2 + hi], gb], [ab])
                        else:
                            av, ab = accs[(m, hi)]
                            tt("dve", gv[:, 0:tn], PS[2 + hi][:, 0:tn], gv[:, 0:tn], ALU.mult, [B_ps[2 + hi], gb], [gb])
                            if i4 < 3:
                                tt("pool", av[:, 0:tn], av[:, 0:tn], gv[:, 0:tn], ALU.add, [ab, gb], [ab])
                            else:
                                tt("pool", mg_v[:, d, t0:t0 + tn], av[:, 0:tn], gv[:, 0:tn], ALU.add, [ab, gb], [mg_b[d][hi]])
        S.barrier()
        xr = Ring(P2["gate"])
        for dq in range(4):
            wo = wload(wsrc(w_out_d[l], 0, 16, dq * 512, 512), 16, 512)
            for m in range(4):
                d = dq * 4 + m
                for hi, (t0, tn) in enumerate(stm.halves):
                    mmg(4 + hi, [(wo[0][:, kc, m * 128:(m + 1) * 128], mg_v[:, kc, t0:t0 + tn]) for kc in range(16)],
                        [wo[1]] + [mg_b[kc][hi] for kc in range(16)], ncol=tn)
                    xo, xob = xold_fn(d, hi, t0, tn, xr)
                    stt("dve", R[:, d, t0:t0 + tn], PS[4 + hi][:, 0:tn], modall[:, l, 32 + d, stm.s:stm.s + 1], xo, ALU.mult, ALU.add,
                        [B_ps[4 + hi], B_mod] + xob, rb(d, hi))

    P2 = {}

    def ffn(stm, l):
        T = stm.T
        TS = stm.halves[0][1]
        R, rb = resid(stm)
        hid = A1.carve(BF16, [8, T], 2, "hid")
        B_hs = [[[Buf("hid%d_%d_%d" % (sl, c, h)) for h in range(2)] for c in range(8)] for sl in range(2)]
        rl = Ring(A1.carve(F32, [TS], 3, "rl"))
        for hb in range(8):
            hv, _ = hid[hb % 2]
            B_hid = B_hs[hb % 2]
            for cq2 in range(2):
                w1 = wload(wsrc(w_ff1_d[l], 0, 16, hb * 1024 + cq2 * 512, 512), 16, 512)
                for m in range(4):
                    c = cq2 * 4 + m
                    for hi, (t0, tn) in enumerate(stm.halves):
                        mmg(0 + hi, [(w1[0][:, kc, m * 128:(m + 1) * 128], hT[:, kc, t0:t0 + tn]) for kc in range(16)], [w1[1]] + hT_reads(hi), ncol=tn)
                        rv, rb_ = rl.next()
                        act(rv[:, 0:tn], PS[0 + hi][:, 0:tn], AF.Relu, [B_ps[0 + hi]], [rb_])
                        tt("pool", hv[:, c, t0:t0 + tn], rv[:, 0:tn], rv[:, 0:tn], ALU.mult, [rb_], [B_hid[c][hi]])
            for dq in range(4):
                w2 = wload(wsrc(w_ff2_d[l], hb * 1024, 8, dq * 512, 512), 8, 512)
                for m in range(4):
                    d = dq * 4 + m
                    for hi, (t0, tn) in enumerate(stm.halves):
                        mmg(2 + hi, [(w2[0][:, kc, m * 128:(m + 1) * 128], hv[:, kc, t0:t0 + tn]) for kc in range(8)],
                            [w2[1]] + [B_hid[kc][hi] for kc in range(8)], ncol=tn)
                        stt("dve", R[:, d, t0:t0 + tn], PS[2 + hi][:, 0:tn], modall[:, l, 80 + d, stm.s:stm.s + 1], R[:, d, t0:t0 + tn],
                            ALU.mult, ALU.add, [B_ps[2 + hi], B_mod] + rb(d, hi), rb(d, hi))

    def spill_x():
        xv = xres_d.rearrange("p (c t) -> p c t", c=16)
        for c4 in range(4):
            rd = []
            for c in range(c4 * 4, c4 * 4 + 4):
                rd += bx(c, 0) + bx(c, 1)
            S.dma("sp", lambda e, c4=c4: e.dma_start(out=xv[:, c4 * 4:(c4 + 1) * 4, :], in_=xT[:, c4 * 4:(c4 + 1) * 4, :]), reads=rd, writes=[B_xres],
                  key="d:xres", part=(c4 != 0))

    def xold_from_dram(d, hi, t0, tn, ring):
        xv = xres_d.rearrange("p (c t) -> p c t", c=16)
        v, b = ring.next()
        dma_in("sp", v[:, 0:tn], xv[:, d, t0:t0 + tn], [b], rb=[B_xres])
        return v[:, 0:tn], [b]

    def xold_resident(stm):
        R, rb = resid(stm)

        def f(d, hi, t0, tn, ring):
            return R[:, d, t0:t0 + tn], rb(d, hi)
        return f

    def final_out():
        sq = Ring(A1.carve(BF16, [512], 3, "sq"))
        rs = A1.carve(F32, [512], 2, "rstd")
        tm = Ring(A1.carve(F32, [512], 4, "tm"))
        og = Ring(A1.carve(F32, [512], 4, "og"))
        fg = pvl(1)[:, P_FIN:P_FIN + 16]
        pr = Ring([0, 1, 2, 3])
        for hi, (t0, tn) in enumerate(SX.halves):
            for c in range(16):
                qv, qb = sq.next()
                act(qv[:, 0:tn], xT[:, c, t0:t0 + tn], AF.Square, bx(c, hi), [qb])
                S.op("pe", lambda e, qv=qv, c=c, tn=tn: e.matmul(PS[6][:, 0:tn], ones_b, qv[:, 0:tn], start=(c == 0), stop=(c == 15)),
                     reads=[qb, B_cb], writes=[B_ps[6]], part=(c != 0))
            rv, rbuf = rs[hi]
            rstd_from(6, tn, 1.0 / D, rv[:, 0:tn], rbuf)
            for c in range(16):
                stt("dve", xT[:, c, t0:t0 + tn], xT[:, c, t0:t0 + tn], fg[:, c:c + 1], rv[:, 0:tn], ALU.mult, ALU.mult,
                    bx(c, hi) + [B_pv, rbuf], bx(c, hi))
        for tt_ in range(TX // 128):
            hi = (tt_ * 128) // 512
            for q4 in range(4):
                pi = pr.next()

                def fn(e, tt_=tt_, q4=q4, pi=pi):
                    for j in range(4):
                        ins = e.transpose(PS[pi][:, j * 128:(j + 1) * 128], xT[:, q4 * 4 + j, tt_ * 128:(tt_ + 1) * 128], ident_f)
                    return ins
                rd = []
                for j in range(4):
                    rd += bx(q4 * 4 + j, hi)
                S.op("pe", fn, reads=rd + [B_cst], writes=[B_ps[pi]])
                ov, ob = og.next()
                if q4 % 2 == 0:
                    cp("dve", ov[:, :], PS[pi][:, :], [B_ps[pi]], [ob])
                else:
                    act(ov[:, :], PS[pi][:, :], AF.Copy, [B_ps[pi]], [ob])
                S.dma("sp", lambda e, ov=ov, tt_=tt_, q4=q4: e.dma_start(out=out_d[tt_ * 128:(tt_ + 1) * 128, q4 * 512:(q4 + 1) * 512], in_=ov[:, :]),
                      reads=[ob], writes=[B_out], key="d:st_" + ob.name, part=True)

    def carve_cq(T):
        cq_holder["v"] = A1.carve(BF16, [4, T], 1, "cq")[0]
        cq_holder["b"] = [[Buf("cq%d_%d_%d" % (A1.gen, c, h)) for h in range(2)] for c in range(4)]

    def carve_z():
        A1.off = 49152 - 16384
        z = A1.carve(F32, [16, TC], 1, "z")[0]
        zT_holder["v"] = z[0]
        zT_holder["b"] = [Buf("z%d_%d" % (A1.gen, c)) for c in range(16)]
        A1.off = 0

    def mixer_phase_carve_p2(TS):
        P2["gate"] = A1.carve(F32, [TS], 4, "gate")
        P2["acc"] = A1.carve(F32, [TS], 4, "acc")

    def ctx_pass(l):
        last = (l == 1)
        S.barrier()
        A1.reset()
        carve_z()
        if l == 0:
            A1.off = 0
            load_rows(SCt, ctx_d)
        else:
            zv = zres_d.rearrange("p (c t) -> p c t", c=16)
            S.dma("sp", lambda e: e.dma_start(out=zT_holder["v"], in_=zv), reads=[B_zres], writes=zT_holder["b"], key="d:zld")
        S.barrier()
        A1.reset()
        A1.n = 49152 - 16384
        ck('c_pre')
        norm_mod(SCt, gs1, 0, l)
        ck('c_norm')
        S.barrier()
        A1.reset()
        phase_kv(SCt, l)
        ck('c_kv')
        if last:
            A1.n = 49152
            return
        S.barrier()
        A1.reset()
        attention(SCt, l)
        ck('c_att')
        S.barrier()
        A1.reset()
        conv(SCt, l)
        ck('c_conv')
        S.barrier()
        A1.reset()
        sgu(SCt, l)
        ck('c_sgu')
        S.barrier()
        A1.reset()
        mg = A1.carve(BF16, [16, TC], 1, "mg")[0]
        mg_b = [[Buf("mgc%d_%d" % (c, h)) for h in range(2)] for c in range(16)]
        mixer_phase_carve_p2(256)
        merge_out(SCt, l, mg[0], mg_b, xold_resident(SCt))
        ck('c_merge')
        S.barrier()
        A1.reset()
        norm_mod(SCt, gs2, 48, l)
        S.barrier()
        A1.reset()
        ffn(SCt, l)
        zv = zres_d.rearrange("p (c t) -> p c t", c=16)
        S.dma("sp", lambda e: e.dma_start(out=zv, in_=zT_holder["v"]), reads=zT_holder["b"], writes=[B_zres], key="d:zst")
        A1.n = 49152

    def x_phaseA(l):
        S.barrier()
        A1.reset()
        norm_mod(SX, gs1, 0, l)
        spill_x()
        S.barrier()
        A1.reset()
        phase_kv(SX, l)
        S.barrier()
        A1.reset()
        conv(SX, l)
        halo_out(l)

    def exchange(l):
        if fused:
            S.barrier()
            rg = [list(range(NCORES))]
            S.dma("pool", lambda e: e.collective_compute("AllGather", ALU.bypass, replica_groups=rg, ins=[kvl_d[l].opt()], outs=[kvall_d[l].opt()]),
                  reads=[B_kvl[l]], writes=[B_kvall[l]], key="d:ag%d" % l, inc=1)

    def x_phaseB(l):
        S.barrier()
        A1.reset()
        sgu(SX, l)
        S.barrier()
        A1.reset()
        ck('xb_sgu')
        attention(SX, l)
        ck('xb_att')
        halo_fix(l)
        ck('xb_halo')
        S.barrier()
        A1.reset()
        mg = A1.carve(BF16, [16, TX], 1, "mg")[0]
        mg_b = [[Buf("mgx%d_%d_%d" % (l, c, h)) for h in range(2)] for c in range(16)]
        mixer_phase_carve_p2(512)
        merge_out(SX, l, mg[0], mg_b, xold_from_dram)
        S.barrier()
        A1.reset()
        norm_mod(SX, gs2, 48, l)
        S.barrier()
        A1.reset()
        ffn(SX, l)

    def ck(name):
        if stop == name:
            raise StopBuild()

    try:
        A1.reset()
        ck('nothing')
        setup()
        ck('setup_a')
        load_pv()
        ck('setup_b')
        setup2()
        ck('setup')
        S.barrier()
        A1.reset()
        adaln(0)
        ck('adaln0')
        adaln(1)
        S.barrier()
        A1.reset()
        layer_consts(0)
        sgu_consts(0)
        ctx_pass(0)
        ck('ctx0')
        S.barrier()
        A1.reset()
        load_rows(SX, x_d)
        ck('loadx')
        x_phaseA(0)
        if nstage >= 2:
            exchange(0)
            ck('ag0')
            x_phaseB(0)
            ck('b0')
            S.barrier()
            layer_consts(1)
            sgu_consts(1)
            S.barrier()
            ctx_pass(1)
            x_phaseA(1)
        if nstage >= 3:
            exchange(1)
            x_phaseB(1)
            S.barrier()
            A1.reset()
            final_out()
    except StopBuild:
        pass
    S.barrier()
    S.emit()
    st.close()
    return nc


def _fm(v):
    v = np.asarray(v, np.float32)
    return np.ascontiguousarray(v.reshape(-1, 128).T)


def _rope_tables():
    def ang(rot_dim):
        axis_dim = rot_dim // 2
        inv = (10000.0 ** (-np.arange(0, axis_dim, 2, dtype=np.float32) / axis_dim)).astype(np.float32)
        t = np.arange(SEQ)
        row = (t // 64).astype(np.float32)
        col = (t % 64).astype(np.float32)
        return row[:, None] * inv, col[:, None] * inv
    out = []
    for rot_dim, rep in ((128, 1), (64, 2)):
        ar, ac = ang(rot_dim)
        a = np.concatenate([ar, ar, ac, ac], axis=1)
        a = np.tile(a, (1, rep))
        out.append(np.cos(a).T.astype(np.float32))
        out.append(np.sin(a).T.astype(np.float32))
    return out


def _rot_mats():
    def rm(half, n=128):
        R = np.zeros((n, n), np.float32)
        for m in range(n):
            if (m // half) % 2 == 0:
                R[m + half, m] = -1.0
            else:
                R[m - half, m] = 1.0
        return R
    return np.stack([rm(32), rm(16)])


_NC_CACHE = {}


def _get_nc(nstage, fused):
    k = (nstage, fused)
    if k not in _NC_CACHE:
        _NC_CACHE[k] = build(nstage, fused)
    return _NC_CACHE[k]


def _common_inputs(inp):
    L = 2
    pvec = np.zeros((L, 128, NP), np.float32)
    for l in range(L):
        pvec[l, :, P_BADA:P_BADA + 96] = _fm(inp["b_ada"][l])
        pvec[l, :, P_GMIX:P_GMIX + 16] = _fm(inp["norm_mix_g"][l])
        pvec[l, :, P_GFFN:P_GFFN + 16] = _fm(inp["norm_ffn_g"][l])
        pvec[l, :, P_BGATE:P_BGATE + 64] = _fm(inp["b_gate"][l])
        for t in range(3):
            pvec[l, :, P_CONV + t * 8:P_CONV + (t + 1) * 8] = _fm(inp["conv_w"][l][t])
        pvec[l, :, P_QG:P_QG + 1] = _fm(inp["q_norm_g"][l])
        pvec[l, :, P_KG:P_KG + 1] = _fm(inp["k_norm_g"][l])
        pvec[l, :, P_MQG:P_MQG + 4] = _fm(inp["mla_q_norm_g"][l])
        pvec[l, :, P_MKG:P_MKG + 4] = _fm(inp["mla_kv_norm_g"][l])
        pvec[l, :, P_FIN:P_FIN + 16] = _fm(inp["final_norm_g"])
    cvec = np.zeros((128, 16, 2), np.float32)
    cvec[:, :, 0] = _fm(inp["c"][0])
    cvec[:, :, 1] = _fm(inp["c_ctx"])
    f = lambda a: np.ascontiguousarray(np.asarray(a, np.float32))
    com = {
        "ctx": f(inp["ctx"][0]), "cvec": cvec.reshape(128, 32), "pvec": pvec, "rot": _rot_mats(),
        "ident": np.eye(128, dtype=np.float32),
        "sgu_ln_g": f(inp["sgu_ln_g"]), "sgu_ln_b": f(inp["sgu_ln_b"]),
        "sgu_b_s": f(np.asarray(inp["sgu_b_s"]).reshape(2, 1024)), "sgu_w_s": f(inp["sgu_w_s"]),
        "w_ada": f(inp["w_ada"]), "w_in": f(inp["w_in"]), "w_uq": f(inp["w_uq"]), "w_ukv": f(inp["w_ukv"]),
        "w_branch": f(inp["w_branch"]), "w_out": f(inp["w_out"]), "w_ff1": f(inp["w_ff1"]), "w_ff2": f(inp["w_ff2"]),
    }
    return com


FUSED = True


def kernel(**inp):
    com = _common_inputs(inp)
    x = np.asarray(inp["x"], np.float32)[0]
    tabs = _rope_tables()
    per = []
    for r in range(NCORES):
        sel = np.zeros((128, 16), np.float32)
        if r > 0:
            sel[:, r - 1] = 1.0
        if r < NCORES - 1:
            sel[:, 8 + r + 1] = 1.0
        d = dict(com)
        d["x"] = np.ascontiguousarray(x[r * TX:(r + 1) * TX])
        d["rope"] = np.ascontiguousarray(np.stack([t[:, r * TX:(r + 1) * TX] for t in tabs]))
        d["sel"] = sel
        per.append(d)
    if FUSED:
        nc = _get_nc(3, True)
        res = run_bass_kernel_spmd(nc, per, core_ids=list(range(NCORES)))
        return np.concatenate([r["out"] for r in res.results], axis=0)[None]
    zkv = np.zeros((NCORES * ROWS, TX), ml_dtypes.bfloat16)
    kv = [zkv, zkv]
    out = None
    for stage in (1, 2, 3):
        nc = _get_nc(stage, False)
        maps = []
        for r in range(NCORES):
            d = dict(per[r])
            d["kvall0"] = kv[0]
            d["kvall1"] = kv[1]
            maps.append(d)
        res = run_bass_kernel_spmd(nc, maps, core_ids=list(range(NCORES)))
        if stage < 3:
            kv[stage - 1] = np.ascontiguousarray(np.concatenate([r["kvl%d" % (stage - 1)] for r in res.results], axis=0))
        else:
            out = np.concatenate([r["out"] for r in res.results], axis=0)[None]
    return out.astype(np.float32)
```
